# Optimizing a Trainium2 kernel written in Bass

```python
import jax, jax.numpy as jnp
from jax import lax
import numpy as np

D_MODEL = 4096
BATCH = 1
SEQ = 8192
DEPTH = 2

HEAD_DIM = 128
N_HEADS_FOX = 12
N_HEADS_SB = 10
N_HEADS_MLA = 10
D_FOX = N_HEADS_FOX * HEAD_DIM
D_SB = N_HEADS_SB * HEAD_DIM
Q_LORA = 1536
KV_LORA = 512
QK_NOPE = 128
QK_ROPE = 64
QK_MLA = QK_NOPE + QK_ROPE
V_MLA = HEAD_DIM
D_MLA = N_HEADS_MLA * V_MLA
D_MIX = D_FOX + D_SB + D_MLA
N_HEADS_TOTAL = N_HEADS_FOX + N_HEADS_SB + N_HEADS_MLA
ROPE_THETA = 10000.0
BLOCK_Q = 128
EPS = 1e-6
IN_SIZES = [D_FOX, D_FOX, D_FOX, N_HEADS_FOX, D_SB, D_SB, D_SB, Q_LORA, KV_LORA, QK_ROPE, D_MIX]
N_IN = int(sum(IN_SIZES))
IN_SPLITS = [int(v) for v in np.cumsum(IN_SIZES)[:-1]]

kernel_name = "hybrid_fox_stickbreak_mla_block"


def rms_norm(x, g):
    xf = x.astype(jnp.float32)
    y = xf * lax.rsqrt(jnp.mean(xf * xf, axis=-1, keepdims=True) + EPS)
    return (y * g.astype(jnp.float32)).astype(x.dtype)


def rope(x, positions):
    half = x.shape[-1] // 2
    inv = ROPE_THETA ** (-jnp.arange(half, dtype=jnp.float32) / half)
    ang = positions.astype(jnp.float32)[..., None] * inv
    cos = jnp.cos(ang)[:, :, None, :]
    sin = jnp.sin(ang)[:, :, None, :]
    xf = x.astype(jnp.float32)
    x1, x2 = xf[..., :half], xf[..., half:]
    return jnp.concatenate([x1 * cos - x2 * sin, x2 * cos + x1 * sin], axis=-1).astype(x.dtype)


def to_heads(t, n_heads, d):
    return t.reshape(t.shape[0], t.shape[1], n_heads, d)


def block_sweep(block_fn, seq):
    out = lax.map(block_fn, jnp.arange(seq // BLOCK_Q))
    nb, b, h, bq, dv = out.shape
    return out.transpose(1, 0, 3, 2, 4).reshape(b, nb * bq, h, dv)


def forgetting_attention(q, k, v, log_f):
    seq = q.shape[2]
    scale = q.shape[-1] ** -0.5
    cum = jnp.cumsum(log_f, axis=-1)
    kpos = jnp.arange(seq)

    def block(i):
        start = i * BLOCK_Q
        qb = lax.dynamic_slice_in_dim(q, start, BLOCK_Q, axis=2)
        cb = lax.dynamic_slice_in_dim(cum, start, BLOCK_Q, axis=2)
        qpos = start + jnp.arange(BLOCK_Q)
        s = jnp.einsum('bhqd,bhkd->bhqk', qb, k, preferred_element_type=jnp.float32) * scale
        s = s + (cb[..., :, None] - cum[..., None, :])
        s = jnp.where(kpos[None, :] <= qpos[:, None], s, -jnp.inf)
        p = jax.nn.softmax(s, axis=-1)
        return jnp.einsum('bhqk,bhkd->bhqd', p.astype(v.dtype), v)

    return block_sweep(block, seq)


def stick_breaking_attention(q, k, v):
    seq = q.shape[2]
    scale = q.shape[-1] ** -0.5
    kpos = jnp.arange(seq)

    def block(i):
        start = i * BLOCK_Q
        qb = lax.dynamic_slice_in_dim(q, start, BLOCK_Q, axis=2)
        qpos = start + jnp.arange(BLOCK_Q)
        causal = kpos[None, :] < qpos[:, None]
        z = jnp.einsum('bhqd,bhkd->bhqk', qb, k, preferred_element_type=jnp.float32) * scale
        log_1m_beta = jnp.where(causal, jax.nn.log_sigmoid(-z), 0.0)
        tail = lax.cumsum(log_1m_beta, axis=3, reverse=True) - log_1m_beta
        log_a = jnp.where(causal, jax.nn.log_sigmoid(z) + tail, -jnp.inf)
        a = jnp.exp(log_a)
        return jnp.einsum('bhqk,bhkd->bhqd', a.astype(v.dtype), v)

    return block_sweep(block, seq)


def causal_softmax_attention(q, k, v):
    seq = q.shape[2]
    scale = q.shape[-1] ** -0.5
    kpos = jnp.arange(seq)

    def block(i):
        start = i * BLOCK_Q
        qb = lax.dynamic_slice_in_dim(q, start, BLOCK_Q, axis=2)
        qpos = start + jnp.arange(BLOCK_Q)
        s = jnp.einsum('bhqd,bhkd->bhqk', qb, k, preferred_element_type=jnp.float32) * scale
        s = jnp.where(kpos[None, :] <= qpos[:, None], s, -jnp.inf)
        p = jax.nn.softmax(s, axis=-1)
        return jnp.einsum('bhqk,bhkd->bhqd', p.astype(v.dtype), v)

    return block_sweep(block, seq)


def setup_inputs(seed: int = 0) -> dict:
    key = jax.random.key(seed)
    ks = jax.random.split(key, 16)
    f32 = jnp.float32

    def nrm(k, shape, scale):
        return jax.random.normal(k, shape, f32) * scale

    def gain(k, shape):
        return 1.0 + 0.02 * jax.random.normal(k, shape, f32)

    x = jax.random.normal(ks[0], (BATCH, SEQ, D_MODEL), f32)
    positions = jnp.broadcast_to(jnp.arange(SEQ, dtype=jnp.int32)[None, :], (BATCH, SEQ))
    return {
        "x": x,
        "positions": positions,
        "norm_in": gain(ks[1], (DEPTH, D_MODEL)),
        "w_in": nrm(ks[2], (DEPTH, D_MODEL, N_IN), D_MODEL ** -0.5),
        "b_f": 3.0 + 0.1 * jax.random.normal(ks[3], (DEPTH, N_HEADS_FOX), f32),
        "q_norm_fox": gain(ks[4], (DEPTH, HEAD_DIM)),
        "k_norm_fox": gain(ks[5], (DEPTH, HEAD_DIM)),
        "cq_norm": gain(ks[6], (DEPTH, Q_LORA)),
        "w_uq": nrm(ks[7], (DEPTH, Q_LORA, N_HEADS_MLA * QK_MLA), Q_LORA ** -0.5),
        "ckv_norm": gain(ks[8], (DEPTH, KV_LORA)),
        "w_ukv": nrm(ks[9], (DEPTH, KV_LORA, N_HEADS_MLA * (QK_NOPE + V_MLA)), KV_LORA ** -0.5),
        "q_norm_mla": gain(ks[10], (DEPTH, QK_MLA)),
        "k_norm_mla": gain(ks[11], (DEPTH, QK_MLA)),
        "out_norm": gain(ks[12], (DEPTH, D_MIX)),
        "w_out": nrm(ks[13], (DEPTH, D_MIX, D_MODEL), D_MIX ** -0.5),
    }


def reference(x, positions, norm_in, w_in, b_f, q_norm_fox, k_norm_fox, cq_norm, w_uq, ckv_norm, w_ukv,
              q_norm_mla, k_norm_mla, out_norm, w_out):
    bsz, seq, _ = x.shape
    for l in range(DEPTH):
        h = rms_norm(x, norm_in[l])
        proj = jnp.einsum('bsd,dn->bsn', h, w_in[l])
        (fq, fk, fv, f_logit, sq, sk, sv, cq, ckv, k_rope_raw, gate) = jnp.split(proj, IN_SPLITS, axis=-1)

        fq = rms_norm(to_heads(fq, N_HEADS_FOX, HEAD_DIM), q_norm_fox[l])
        fk = rms_norm(to_heads(fk, N_HEADS_FOX, HEAD_DIM), k_norm_fox[l])
        fv = to_heads(fv, N_HEADS_FOX, HEAD_DIM)
        log_f = jax.nn.log_sigmoid(f_logit.astype(jnp.float32) + b_f[l].astype(jnp.float32))
        o_fox = forgetting_attention(fq.transpose(0, 2, 1, 3), fk.transpose(0, 2, 1, 3),
                                     fv.transpose(0, 2, 1, 3), log_f.transpose(0, 2, 1))

        sq = to_heads(sq, N_HEADS_SB, HEAD_DIM).transpose(0, 2, 1, 3)
        sk = to_heads(sk, N_HEADS_SB, HEAD_DIM).transpose(0, 2, 1, 3)
        sv = to_heads(sv, N_HEADS_SB, HEAD_DIM).transpose(0, 2, 1, 3)
        o_sb = stick_breaking_attention(sq, sk, sv)

        cq = rms_norm(cq, cq_norm[l])
        mq = to_heads(jnp.einsum('bsr,rn->bsn', cq, w_uq[l]), N_HEADS_MLA, QK_MLA)
        mq = rms_norm(mq, q_norm_mla[l])
        mq = jnp.concatenate([mq[..., :QK_NOPE], rope(mq[..., QK_NOPE:], positions)], axis=-1)
        ckv = rms_norm(ckv, ckv_norm[l])
        kv = to_heads(jnp.einsum('bsr,rn->bsn', ckv, w_ukv[l]), N_HEADS_MLA, QK_NOPE + V_MLA)
        k_nope, mv = kv[..., :QK_NOPE], kv[..., QK_NOPE:]
        k_rope = jnp.broadcast_to(k_rope_raw[:, :, None, :], (bsz, seq, N_HEADS_MLA, QK_ROPE))
        mk = rms_norm(jnp.concatenate([k_nope, k_rope], axis=-1), k_norm_mla[l])
        mk = jnp.concatenate([mk[..., :QK_NOPE], rope(mk[..., QK_NOPE:], positions)], axis=-1)
        o_mla = causal_softmax_attention(mq.transpose(0, 2, 1, 3), mk.transpose(0, 2, 1, 3),
                                         mv.transpose(0, 2, 1, 3))

        o = jnp.concatenate([o_fox, o_sb, o_mla], axis=2)
        o = rms_norm(o, out_norm[l].reshape(N_HEADS_TOTAL, HEAD_DIM)).reshape(bsz, seq, D_MIX)
        o = o * jax.nn.silu(gate)
        x = x + jnp.einsum('bsm,md->bsd', o, w_out[l])
    return x
```

```python
import math
import contextlib
import numpy as np
import ml_dtypes
import concourse.bass as bass
import concourse.mybir as mybir
from concourse.bass_utils import run_bass_kernel_spmd

F32 = mybir.dt.float32
BF16 = mybir.dt.bfloat16
I32 = mybir.dt.int32
AF = mybir.ActivationFunctionType
ALU = mybir.AluOpType
AX = mybir.AxisListType

PE, ACT, DVE, POOL, SP = "pe", "act", "dve", "pool", "sp"
ENGS = [PE, ACT, DVE, POOL, SP]

NCORES = 8
D = 4096
NCH = D // 128
HD = 128
NF, NS, NM = 12, 10, 10
NH = NF + NS + NM
QL, KVL, ROPE = 1536, 512, 64
EPS = 1e-6
C_FQ = 0
C_FK = C_FQ + NF * HD
C_FV = C_FK + NF * HD
C_FL = C_FV + NF * HD
C_SQ = C_FL + NF
C_SK = C_SQ + NS * HD
C_SV = C_SK + NS * HD
C_CQ = C_SV + NS * HD
C_CKV = C_CQ + QL
C_KR = C_CKV + KVL
C_G = C_KR + ROPE
N_IN = C_G + D
MASKVAL = -30000.0


class Buf:
    __slots__ = ("name", "w", "r")

    def __init__(self, name=""):
        self.name = name
        self.w = None
        self.r = []


class Prog:
    def __init__(self, nc):
        self.nc = nc
        self.ops = {e: [] for e in ENGS}
        self.cnt = {}
        self.waited = {}
        self.sems = {}
        self.nops = 0

    def op(self, eng, emit, reads=(), writes=(), dma_key=None, n_dma=1):
        need = {}

        def add(tok, raw):
            key, val, teng, is_dma = tok
            if (not is_dma) and teng == eng and not raw:
                return
            if need.get(key, 0) < val:
                need[key] = val

        for b in reads:
            if b.w is not None:
                add(b.w, True)
        for b in writes:
            if b.w is not None:
                add(b.w, False)
            for t in b.r:
                add(t, False)
        waits = []
        for key, val in need.items():
            if self.waited.get((eng, key), 0) < val:
                self.waited[(eng, key)] = val
                waits.append((key, val))
        if dma_key is not None:
            key = ("dma", dma_key)
            self.cnt[key] = self.cnt.get(key, 0) + 16 * n_dma
            tok = (key, self.cnt[key], eng, True)
            inc = 16
        else:
            key = ("eng", eng)
            self.cnt[key] = self.cnt.get(key, 0) + 1
            tok = (key, self.cnt[key], eng, False)
            inc = 1
        self.ops[eng].append((waits, emit, key, inc))
        for b in writes:
            b.w = tok
            b.r = []
        for b in reads:
            b.r.append(tok)
        self.nops += 1
        return tok

    def barrier(self):
        for eng in ENGS:
            waits = []
            for key, val in self.cnt.items():
                if self.waited.get((eng, key), 0) < val:
                    self.waited[(eng, key)] = val
                    waits.append((key, val))
            self.ops[eng].append((waits, None, None, 0))

    def flush(self):
        nc = self.nc
        for key in self.cnt:
            if key not in self.sems:
                self.sems[key] = nc.alloc_semaphore("s%d" % len(self.sems))
        sems = self.sems
        ops = self.ops
        self.ops = {e: [] for e in ENGS}
        with nc.Block() as block:
            def run(eng_name):
                def body(e):
                    for (waits, emit, key, inc) in ops[eng_name]:
                        for (k, v) in waits:
                            e.wait_ge(sems[k], v)
                        if emit is None:
                            continue
                        r = emit(e)
                        if isinstance(r, (list, tuple)):
                            for x in r:
                                x.then_inc(sems[key], inc)
                        else:
                            r.then_inc(sems[key], inc)
                return body

            block.tensor(run(PE))
            block.scalar(run(ACT))
            block.vector(run(DVE))
            block.gpsimd(run(POOL))
            block.sync(run(SP))


_UID = [0]


def uname(name):
    _UID[0] += 1
    return "%s_u%d" % (name, _UID[0])


class Ring:
    def __init__(self, st, nc, name, shape, dtype, n, psum=False):
        mk = nc.psum_tensor if psum else nc.sbuf_tensor
        self.t = [st.enter_context(mk(uname("%s%d" % (name, i)), shape, dtype)) for i in range(n)]
        self.b = [Buf("%s%d" % (name, i)) for i in range(n)]
        self.i = 0

    def next(self):
        i = self.i
        self.i = (i + 1) % len(self.t)
        return self.t[i], self.b[i]


class Cfg:
    def __init__(self, nj):
        self.NJ = nj
        self.T = nj * 128
        self.S = self.T * NCORES
        self.NB = nj * NCORES
        self.pieces = [(t0, min(512, self.T - t0)) for t0 in range(0, self.T, 512)]


def phase1(nc, P, cfg, d):
    T = cfg.T
    NJ = cfg.NJ
    pieces = cfg.pieces
    w_in = d["w_in"]
    with contextlib.ExitStack() as st:
        def sb(name, shape, dt):
            return st.enter_context(nc.sbuf_tensor(uname(name), shape, dt))

        def const(name, shape, dt):
            t = sb("c_" + name, shape, dt)
            b = Buf(name)
            P.op(SP, lambda e: e.dma_start(out=t[:], in_=d[name]), writes=[b], dma_key="c_" + name)
            return t, b

        ones_f, b_ones = const("ones_f", [128, 128], F32)
        gin, b_gin = const("gin", [128, NCH], F32)
        gq_f, b_gqf = const("gq_f", [128, 1], F32)
        gk_f, b_gkf = const("gk_f", [128, 1], F32)
        bf_b, b_bfb = const("bf_b", [128, NF], F32)

        eps_t = sb("eps_t", [128, 4], F32)
        b_eps = Buf("eps")
        P.op(DVE, lambda e: e.memset(eps_t[:, 0:1], EPS), writes=[b_eps])
        P.op(DVE, lambda e: e.memset(eps_t[:, 1:2], EPS * HD), writes=[b_eps])
        P.op(DVE, lambda e: e.memset(eps_t[:, 2:3], 1.0), writes=[b_eps])
        hT = sb("hT", [128, NCH, T], BF16)
        b_hT = Buf("hT")
        rstd = sb("rstd", [128, T], F32)
        b_rstd = Buf("rstd")
        tmpf = sb("tmpf", [128, T], F32)
        b_tmpf = Buf("tmpf")
        xs = Ring(st, nc, "xs", [128, T], F32, 2)
        sq = Ring(st, nc, "sq", [128, T], F32, 2)
        raw = Ring(st, nc, "raw", [128, T], F32, 2)
        obf = Ring(st, nc, "obf", [128, T], BF16, 3)
        vst = Ring(st, nc, "vst", [128, 512], BF16, 3)
        WC = 256
        wb = Ring(st, nc, "wb", [128, NCH, WC], BF16, 2)
        ps = Ring(st, nc, "ps", [128, 512], F32, 4, psum=True)
        pss = Ring(st, nc, "pss", [128, 512], F32, 2, psum=True)

        xT = d["xT"]

        ssq = [pss.next() for _ in pieces]
        for c in range(NCH):
            xt, xb = xs.next()
            P.op(SP, lambda e, xt=xt, c=c: e.dma_start(out=xt[:], in_=xT[:, c, :]), writes=[xb], dma_key="xs%d" % (c % 2))
            st_, sbf = sq.next()
            P.op(ACT, lambda e, st_=st_, xt=xt: e.activation(out=st_[:], in_=xt[:], func=AF.Square), reads=[xb], writes=[sbf])
            for pi, (t0, n) in enumerate(pieces):
                pt, pb = ssq[pi]
                P.op(PE, lambda e, pt=pt, st_=st_, t0=t0, n=n, c=c: e.matmul(
                    pt[:, :n], ones_f[:], st_[:, t0:t0 + n], start=(c == 0), stop=(c == NCH - 1), skip_group_check=True),
                    reads=[b_ones, sbf], writes=[pb])
        for pi, (t0, n) in enumerate(pieces):
            pt, pb = ssq[pi]
            P.op(ACT, lambda e, pt=pt, t0=t0, n=n: e.activation(
                out=tmpf[:, t0:t0 + n], in_=pt[:, :n], func=AF.Sqrt, scale=1.0 / D, bias=eps_t[:, 0:1]),
                reads=[pb, b_eps], writes=[b_tmpf])
        P.op(DVE, lambda e: e.reciprocal(out=rstd[:], in_=tmpf[:]), reads=[b_tmpf], writes=[b_rstd])
        for c in range(NCH):
            xt, xb = xs.next()
            P.op(SP, lambda e, xt=xt, c=c: e.dma_start(out=xt[:], in_=xT[:, c, :]), writes=[xb], dma_key="xs%d" % (c % 2))
            P.op(DVE, lambda e, xt=xt, c=c: e.scalar_tensor_tensor(
                out=hT[:, c, :], in0=xt[:], scalar=gin[:, c:c + 1], in1=rstd[:], op0=ALU.mult, op1=ALU.mult),
                reads=[xb, b_gin, b_rstd], writes=[b_hT])

        def load_w(col0, ncols):
            wt, wbuf = wb.next()
            P.op(POOL, lambda e, wt=wt: e.dma_start(
                out=wt[:, :, :ncols], in_=w_in[:, col0:col0 + ncols].rearrange("(c p) n -> p c n", p=128)),
                writes=[wbuf], dma_key="wb%d" % (wb.i))
            return wt, wbuf

        def proj_fm(wt, wbuf, c0, ncols, consumer):
            for pi, (t0, n) in enumerate(pieces):
                pt, pb = ps.next()
                for c in range(NCH):
                    P.op(PE, lambda e, pt=pt, c=c, t0=t0, n=n: e.matmul(
                        pt[:ncols, :n], wt[:, c, c0:c0 + ncols], hT[:, c, t0:t0 + n],
                        start=(c == 0), stop=(c == NCH - 1), skip_group_check=True),
                        reads=[wbuf, b_hT], writes=[pb])
                consumer(pt, pb, pi, t0, n)

        def proj_tm(wt, wbuf, c0, ncols, consumer):
            for j in range(NJ):
                pt, pb = ps.next()
                for c in range(NCH):
                    P.op(PE, lambda e, pt=pt, c=c, j=j: e.matmul(
                        pt[:, :ncols], hT[:, c, j * 128:(j + 1) * 128], wt[:, c, c0:c0 + ncols],
                        start=(c == 0), stop=(c == NCH - 1), skip_group_check=True),
                        reads=[wbuf, b_hT], writes=[pb])
                consumer(pt, pb, j)

        def headnorm_store(rt, rb, gain, b_gain, scale, dst):
            st_, sbf = sq.next()
            P.op(ACT, lambda e: e.activation(out=st_[:], in_=rt[:], func=AF.Square), reads=[rb], writes=[sbf])
            sp_ = [pss.next() for _ in pieces]
            for pi, (t0, n) in enumerate(pieces):
                pt, pb = sp_[pi]
                P.op(PE, lambda e, pt=pt, t0=t0, n=n: e.matmul(pt[:, :n], ones_f[:], st_[:, t0:t0 + n], start=True, stop=True,
                                                             skip_group_check=True),
                     reads=[b_ones, sbf], writes=[pb])
                P.op(ACT, lambda e, pt=pt, t0=t0, n=n: e.activation(
                    out=st_[:, t0:t0 + n], in_=pt[:, :n], func=AF.Sqrt, scale=1.0 / (HD * scale * scale),
                    bias=eps_t[:, (1 if scale != 1.0 else 0):(2 if scale != 1.0 else 1)]),
                    reads=[pb, b_eps], writes=[sbf])
            P.op(DVE, lambda e: e.reciprocal(out=st_[:], in_=st_[:]), reads=[sbf], writes=[sbf])
            ot, ob = obf.next()
            P.op(DVE, lambda e: e.scalar_tensor_tensor(out=ot[:], in0=rt[:], scalar=gain[:, 0:1], in1=st_[:],
                                                       op0=ALU.mult, op1=ALU.mult),
                 reads=[rb, sbf, b_gain], writes=[ob])
            P.op(SP, lambda e: e.dma_start(out=dst, in_=ot[:]), reads=[ob], writes=[Buf()], dma_key="st_obf%d" % obf.i)

        def seg_cols(c_start, ncols_total, fn, group=128):
            off = 0
            while off < ncols_total:
                n = min(WC, ncols_total - off)
                wt, wbuf = load_w(c_start + off, n)
                g0 = 0
                while g0 < n:
                    gn = min(group, n - g0)
                    fn(wt, wbuf, g0, gn, (off + g0))
                    g0 += gn
                off += n

        def fox_qk(dst_all, gain, b_gain, scale):
            def fn(wt, wbuf, g0, gn, segoff):
                h = segoff // HD
                rt, rb = raw.next()

                def cons(pt, pb, pi, t0, n):
                    P.op(ACT, lambda e: e.activation(out=rt[:, t0:t0 + n], in_=pt[:, :n], func=AF.Copy), reads=[pb], writes=[rb])
                proj_fm(wt, wbuf, g0, gn, cons)
                headnorm_store(rt, rb, gain, b_gain, scale, dst_all[h])
            return fn

        seg_cols(C_FQ, NF * HD, fox_qk(d["qn"], gq_f, b_gqf, HD ** -0.5))
        seg_cols(C_FK, NF * HD, fox_qk(d["kn"], gk_f, b_gkf, 1.0))

        def v_seg(head0):
            def fn(wt, wbuf, g0, gn, segoff):
                nh = gn // HD
                h0 = head0 + segoff // HD

                def cons(pt, pb, j):
                    vt, vb = vst.next()
                    P.op(ACT, lambda e: e.activation(out=vt[:, :gn], in_=pt[:, :gn], func=AF.Copy), reads=[pb], writes=[vb])
                    P.op(SP, lambda e: e.dma_start(
                        out=d["v"][h0:h0 + nh, j].rearrange("h p v -> p h v"),
                        in_=vt[:, :gn].rearrange("p (h v) -> p h v", h=nh)),
                        reads=[vb], writes=[Buf()], dma_key="st_v%d" % vst.i)
                proj_tm(wt, wbuf, g0, gn, cons)
            return fn

        seg_cols(C_FV, NF * HD, v_seg(0), group=WC)

        lft = sb("lft", [128, NJ, NF], F32)
        b_lft = Buf("lft")

        def fl_fn(wt, wbuf, g0, gn, segoff):
            def cons(pt, pb, j):
                P.op(DVE, lambda e: e.tensor_tensor(out=lft[:, j, :], in0=pt[:, :NF], in1=bf_b[:], op=ALU.add),
                     reads=[pb, b_bfb], writes=[b_lft])
            proj_tm(wt, wbuf, g0, gn, cons)
        seg_cols(C_FL, NF, fl_fn)
        P.op(ACT, lambda e: e.activation(out=lft[:], in_=lft[:], func=AF.Exp, scale=-1.0), reads=[b_lft], writes=[b_lft])
        P.op(ACT, lambda e: e.activation(out=lft[:], in_=lft[:], func=AF.Ln, bias=eps_t[:, 2:3]), reads=[b_lft, b_eps], writes=[b_lft])
        P.op(DVE, lambda e: e.tensor_scalar(out=lft[:], in0=lft[:], scalar1=-1.0, scalar2=None, op0=ALU.mult),
             reads=[b_lft], writes=[b_lft])
        P.op(SP, lambda e: e.dma_start(out=d["lf"].rearrange("j p h -> p j h"), in_=lft[:]), reads=[b_lft], writes=[Buf()],
             dma_key="st_lf")

        def sb_qk(dst_all, scale):
            def fn(wt, wbuf, g0, gn, segoff):
                h = NF + segoff // HD
                ot, ob = obf.next()

                def cons(pt, pb, pi, t0, n):
                    P.op(ACT, lambda e: e.activation(out=ot[:, t0:t0 + n], in_=pt[:, :n], func=AF.Copy, scale=scale),
                         reads=[pb], writes=[ob])
                proj_fm(wt, wbuf, g0, gn, cons)
                P.op(SP, lambda e: e.dma_start(out=dst_all[h], in_=ot[:]), reads=[ob], writes=[Buf()], dma_key="st_obf%d" % obf.i)
            return fn

        seg_cols(C_SQ, NS * HD, sb_qk(d["qn"], HD ** -0.5))
        seg_cols(C_SK, NS * HD, sb_qk(d["kn"], 1.0))
        seg_cols(C_SV, NS * HD, v_seg(NF), group=WC)

        def raw_seg(dst_all):
            def fn(wt, wbuf, g0, gn, segoff):
                gi = segoff // 128
                rt, rb = raw.next()

                def cons(pt, pb, pi, t0, n):
                    P.op(ACT, lambda e: e.activation(out=rt[:gn, t0:t0 + n], in_=pt[:gn, :n], func=AF.Copy), reads=[pb], writes=[rb])
                proj_fm(wt, wbuf, g0, gn, cons)
                P.op(SP, lambda e: e.dma_start(out=dst_all[gi][:gn], in_=rt[:gn, :]), reads=[rb], writes=[Buf()],
                     dma_key="st_raw%d" % raw.i)
            return fn

        seg_cols(C_CQ, QL, raw_seg(d["cq_raw"]))
        seg_cols(C_CKV, KVL, raw_seg(d["ckv_raw"]))
        seg_cols(C_KR, ROPE, raw_seg(d["kr_raw"]))

        def gate_fn(wt, wbuf, g0, gn, segoff):
            gi = segoff // 128
            ot, ob = obf.next()

            def cons(pt, pb, pi, t0, n):
                P.op(ACT, lambda e: e.activation(out=ot[:, t0:t0 + n], in_=pt[:, :n], func=AF.Silu), reads=[pb], writes=[ob])
            proj_fm(wt, wbuf, g0, gn, cons)
            P.op(SP, lambda e: e.dma_start(out=d["gate"][gi], in_=ot[:]), reads=[ob], writes=[Buf()], dma_key="st_obf%d" % obf.i)
        seg_cols(C_G, D, gate_fn)

        P.barrier()
        P.flush()


def shapes(cfg):
    T, NJ = cfg.T, cfg.NJ
    return {
        "xT": ([128, NCH, T], F32), "pos": ([1, T], I32),
        "w_in": ([D, N_IN], F32), "w_uq": ([QL, NM * 192], F32), "w_ukv": ([KVL, NM * 256], F32),
        "w_out": ([D, D], F32),
        "gin": ([128, NCH], F32), "gq_f": ([128, 1], F32), "gk_f": ([128, 1], F32), "bf_b": ([128, NF], F32),
        "gcq": ([128, QL // 128], F32), "gckv": ([128, KVL // 128], F32),
        "gqm_n": ([128, 1], F32), "gqm_r": ([64, 1], F32), "gkm_n": ([128, 1], F32), "gkm_r": ([64, 1], F32),
        "gout": ([128, NH], F32),
        "ones_f": ([128, 128], F32), "ones_b": ([128, 128], BF16), "ident_b": ([128, 128], BF16),
        "rotT": ([64, 64], F32), "invf": ([64, 1], F32), "tri_f": ([128, 128], F32), "bstrict": ([128, 128], F32),
        "ident_f": ([128, 128], F32), "negT_b": ([128, 128], BF16),
        "mask_le": ([128, NCORES, 128], BF16), "mask_lt": ([128, NCORES, 128], BF16), "sel": ([128, NCORES], F32),
        "qn": ([NH, 128, T], BF16), "qr": ([NM, 64, T], BF16), "gate": ([NH, 128, T], BF16),
        "cq_raw": ([QL // 128, 128, T], F32), "ckv_raw": ([KVL // 128, 128, T], F32), "kr_raw": ([1, 128, T], F32),
        "kn": ([NH, 128, T], BF16), "kr": ([NM, 64, T], BF16), "v": ([NH, NJ, 128, 128], BF16), "lf": ([NJ, 128, NF], F32),
        "kn_all": ([NCORES, NH, 128, T], BF16), "kr_all": ([NCORES, NM, 64, T], BF16),
        "v_all": ([NCORES, NH, NJ, 128, 128], BF16), "lf_all": ([NCORES, NJ, 128, NF], F32),
        "xT_out": ([128, NCH, T], F32),
        "kn_g": ([NH, 128, cfg.S], BF16), "kr_g": ([NM, 64, cfg.S], BF16),
        "v_g": ([NH, cfg.NB, 128, 128], BF16), "lf_g": ([cfg.NB, 128, NF], F32),
    }


NP_DT = {F32: np.float32, BF16: ml_dtypes.bfloat16, I32: np.int32}


def declare(nc, cfg, ins, outs, internal=()):
    sh = shapes(cfg)
    d = {}
    for n in ins:
        d[n] = nc.dram_tensor(n, sh[n][0], sh[n][1], kind="ExternalInput").ap()
    for n in outs:
        d[n] = nc.dram_tensor(n, sh[n][0], sh[n][1], kind="ExternalOutput").ap()
    for n in internal:
        d[n] = nc.dram_tensor(n, sh[n][0], sh[n][1]).ap()
    return d


def host_consts(cfg):
    c = {}
    c["ones_f"] = np.ones((128, 128), np.float32)
    c["ones_b"] = np.ones((128, 128), ml_dtypes.bfloat16)
    c["ident_b"] = np.eye(128, dtype=np.float32).astype(ml_dtypes.bfloat16)
    c["ident_f"] = np.eye(128, dtype=np.float32)
    rT = np.zeros((64, 64), np.float32)
    for m in range(32):
        rT[m + 32, m] = -1.0
        rT[m, m + 32] = 1.0
    c["rotT"] = rT
    half = 32
    inv = (10000.0 ** (-np.arange(half, dtype=np.float32) / half)).astype(np.float32)
    c["invf"] = np.concatenate([inv, inv]).reshape(64, 1).astype(np.float32)
    p = np.arange(128)
    c["tri_f"] = (p[:, None] <= p[None, :]).astype(np.float32)
    c["bstrict"] = (p[:, None] < p[None, :]).astype(np.float32)
    c["negT_b"] = (-(p[:, None] >= p[None, :]).astype(np.float32)).astype(ml_dtypes.bfloat16)
    return c


def core_consts(cfg, c):
    p = np.arange(128)
    le = np.zeros((128, NCORES, 128), np.float32)
    lt = np.zeros((128, NCORES, 128), np.float32)
    for cp in range(NCORES):
        if cp < c:
            pass
        elif cp == c:
            le[:, cp, :] = np.where(p[:, None] <= p[None, :], 0.0, MASKVAL)
            lt[:, cp, :] = np.where(p[:, None] < p[None, :], 0.0, MASKVAL)
        else:
            le[:, cp, :] = MASKVAL
            lt[:, cp, :] = MASKVAL
    sel = np.zeros((128, NCORES), np.float32)
    sel[:, c] = 1.0
    return {"mask_le": le.astype(ml_dtypes.bfloat16), "mask_lt": lt.astype(ml_dtypes.bfloat16), "sel": sel}


def own_tokens(cfg, c):
    return np.concatenate([np.arange((NCORES * j + c) * 128, (NCORES * j + c + 1) * 128) for j in range(cfg.NJ)])


def layer_weights(inp, l):
    g = {}
    g["w_in"] = np.ascontiguousarray(inp["w_in"][l])
    g["w_uq"] = np.ascontiguousarray(inp["w_uq"][l])
    g["w_ukv"] = np.ascontiguousarray(inp["w_ukv"][l])
    g["w_out"] = np.ascontiguousarray(inp["w_out"][l])
    g["gin"] = np.ascontiguousarray(inp["norm_in"][l].reshape(NCH, 128).T)
    g["gq_f"] = np.ascontiguousarray(inp["q_norm_fox"][l].reshape(128, 1))
    g["gk_f"] = np.ascontiguousarray(inp["k_norm_fox"][l].reshape(128, 1))
    g["bf_b"] = np.ascontiguousarray(np.broadcast_to(inp["b_f"][l][None, :], (128, NF)))
    g["gcq"] = np.ascontiguousarray(inp["cq_norm"][l].reshape(QL // 128, 128).T)
    g["gckv"] = np.ascontiguousarray(inp["ckv_norm"][l].reshape(KVL // 128, 128).T)
    g["gqm_n"] = np.ascontiguousarray(inp["q_norm_mla"][l][:128].reshape(128, 1))
    g["gqm_r"] = np.ascontiguousarray(inp["q_norm_mla"][l][128:].reshape(64, 1))
    g["gkm_n"] = np.ascontiguousarray(inp["k_norm_mla"][l][:128].reshape(128, 1))
    g["gkm_r"] = np.ascontiguousarray(inp["k_norm_mla"][l][128:].reshape(64, 1))
    g["gout"] = np.ascontiguousarray(inp["out_norm"][l].reshape(NH, 128).T)
    return g


def phase1b(nc, P, cfg, d):
    T, NJ, pieces = cfg.T, cfg.NJ, cfg.pieces
    NQ, NK = QL // 128, KVL // 128
    with contextlib.ExitStack() as st:
        def sb(name, shape, dt):
            return st.enter_context(nc.sbuf_tensor(uname(name), shape, dt))

        def const(name, shape, dt, eng=SP):
            t = sb("c_" + name, shape, dt)
            b = Buf(name)
            P.op(eng, lambda e: e.dma_start(out=t[:], in_=d[name]), writes=[b], dma_key="c_" + name)
            return t, b

        ones_f, b_ones = const("ones_f", [128, 128], F32)
        rotT, b_rot = const("rotT", [64, 64], F32)
        invf, b_invf = const("invf", [64, 1], F32)
        gcq, b_gcq = const("gcq", [128, NQ], F32)
        gckv, b_gckv = const("gckv", [128, NK], F32)
        gqn, b_gqn = const("gqm_n", [128, 1], F32)
        gqr, b_gqr = const("gqm_r", [64, 1], F32)
        gkn, b_gkn = const("gkm_n", [128, 1], F32)
        gkr, b_gkr = const("gkm_r", [64, 1], F32)
        eps_t = sb("eps_t", [128, 2], F32)
        b_eps = Buf("eps")
        P.op(DVE, lambda e: e.memset(eps_t[:, 0:1], EPS), writes=[b_eps])
        P.op(DVE, lambda e: e.memset(eps_t[:, 1:2], EPS * 192.0), writes=[b_eps])

        sq = Ring(st, nc, "sq", [128, T], F32, 2)
        rawn = Ring(st, nc, "rawn", [128, T], F32, 2)
        rawr = Ring(st, nc, "rawr", [64, T], F32, 2)
        rs = Ring(st, nc, "rs", [128, T], F32, 2)
        obf = Ring(st, nc, "obf", [128, T], BF16, 2)
        obr = Ring(st, nc, "obr", [64, T], BF16, 2)
        yr = Ring(st, nc, "yr", [64, T], F32, 2)
        t1 = Ring(st, nc, "t1", [64, T], F32, 2)
        vst = Ring(st, nc, "vst", [128, 128], BF16, 3)
        ps = Ring(st, nc, "ps", [128, 512], F32, 4, psum=True)
        pss = Ring(st, nc, "pss", [128, 512], F32, 2, psum=True)

        posi = sb("posi", [64, T], I32)
        ang = sb("ang", [64, T], F32)
        sin_t = sb("sin_t", [64, T], F32)
        cos_t = sb("cos_t", [64, T], F32)
        b_posi, b_ang, b_sin, b_cos = Buf(), Buf(), Buf(), Buf()
        P.op(SP, lambda e: e.dma_start(out=posi[:], in_=d["pos"].partition_broadcast(64)), writes=[b_posi], dma_key="pos")
        P.op(DVE, lambda e: e.tensor_copy(out=ang[:], in_=posi[:]), reads=[b_posi], writes=[b_ang])
        P.op(DVE, lambda e: e.tensor_scalar(out=ang[:], in0=ang[:], scalar1=invf[:, 0:1], scalar2=None, op0=ALU.mult),
             reads=[b_ang, b_invf], writes=[b_ang])
        MAGIC = 12582912.0
        pif = posi[:].bitcast(F32)

        def trig(dst, bdst, shift):
            P.op(DVE, lambda e: e.tensor_scalar(out=dst[:], in0=ang[:], scalar1=1.0 / (2 * math.pi), scalar2=shift,
                                                op0=ALU.mult, op1=ALU.add), reads=[b_ang], writes=[bdst])
            P.op(DVE, lambda e: e.tensor_scalar(out=pif, in0=dst[:], scalar1=MAGIC, scalar2=None, op0=ALU.add),
                 reads=[bdst], writes=[b_posi])
            P.op(DVE, lambda e: e.tensor_scalar(out=pif, in0=pif, scalar1=MAGIC, scalar2=None, op0=ALU.subtract),
                 reads=[b_posi], writes=[b_posi])
            P.op(DVE, lambda e: e.tensor_tensor(out=dst[:], in0=dst[:], in1=pif, op=ALU.subtract),
                 reads=[bdst, b_posi], writes=[bdst])
            P.op(ACT, lambda e: e.activation(out=dst[:], in_=dst[:], func=AF.Sin, scale=2 * math.pi), reads=[bdst], writes=[bdst])
        trig(sin_t, b_sin, 0.0)
        trig(cos_t, b_cos, 0.25)

        lr_raw = sb("lr_raw", [128, NQ, T], F32)
        b_lr_raw = Buf()

        def lowrank_norm(src, ng, gains, b_g, name):
            rawt = lr_raw
            outn = sb(name + "_n", [128, ng, T], BF16)
            b_raw, b_out = b_lr_raw, Buf()
            P.op(SP, lambda e: e.dma_start(out=rawt[:, 0:ng, :], in_=src.rearrange("g p t -> p g t")), writes=[b_raw], dma_key="lr_ld")
            acc = [pss.next() for _ in pieces]
            for g in range(ng):
                st_, sbf = sq.next()
                P.op(ACT, lambda e, st_=st_, g=g: e.activation(out=st_[:], in_=rawt[:, g, :], func=AF.Square), reads=[b_raw], writes=[sbf])
                for pi, (t0, n) in enumerate(pieces):
                    pt, pb = acc[pi]
                    P.op(PE, lambda e, pt=pt, st_=st_, t0=t0, n=n, g=g: e.matmul(
                        pt[:, :n], ones_f[:], st_[:, t0:t0 + n], start=(g == 0), stop=(g == ng - 1), skip_group_check=True),
                        reads=[b_ones, sbf], writes=[pb])
            rt, rb = rs.next()
            for pi, (t0, n) in enumerate(pieces):
                pt, pb = acc[pi]
                P.op(ACT, lambda e, pt=pt, t0=t0, n=n: e.activation(out=rt[:, t0:t0 + n], in_=pt[:, :n], func=AF.Sqrt,
                                                                     scale=1.0 / (ng * 128), bias=eps_t[:, 0:1]),
                     reads=[pb, b_eps], writes=[rb])
            P.op(DVE, lambda e: e.reciprocal(out=rt[:], in_=rt[:]), reads=[rb], writes=[rb])
            for g in range(ng):
                P.op(DVE, lambda e, g=g: e.scalar_tensor_tensor(out=outn[:, g, :], in0=rawt[:, g, :], scalar=gains[:, g:g + 1],
                                                               in1=rt[:], op0=ALU.mult, op1=ALU.mult),
                     reads=[b_raw, rb, b_g], writes=[b_out])
            return outn, b_out

        cqn, b_cqn = lowrank_norm(d["cq_raw"], NQ, gcq, b_gcq, "cq")
        ckvn, b_ckvn = lowrank_norm(d["ckv_raw"], NK, gckv, b_gckv, "ckv")

        krr = sb("krr", [64, T], F32)
        sqkr = sb("sqkr", [64, T], F32)
        b_krr, b_sqkr = Buf(), Buf()
        P.op(SP, lambda e: e.dma_start(out=krr[:], in_=d["kr_raw"][0, 0:64, :]), writes=[b_krr], dma_key="krr")
        P.op(ACT, lambda e: e.activation(out=sqkr[:], in_=krr[:], func=AF.Square), reads=[b_krr], writes=[b_sqkr])

        wq = Ring(st, nc, "wq", [128, NQ, 192], BF16, 2)
        wkv = sb("wkv", [128, NK, NM * 256], BF16)
        b_wkv = Buf()
        P.op(POOL, lambda e: e.dma_start(out=wkv[:], in_=d["w_ukv"].rearrange("(c p) n -> p c n", p=128)), writes=[b_wkv], dma_key="wkv")

        def up_fm(wt, wbuf, c0, ncols, src, b_src, ng, dst_t, dst_b):
            for pi, (t0, n) in enumerate(pieces):
                pt, pb = ps.next()
                for g in range(ng):
                    P.op(PE, lambda e, pt=pt, g=g, t0=t0, n=n: e.matmul(
                        pt[:ncols, :n], wt[:, g, c0:c0 + ncols], src[:, g, t0:t0 + n], start=(g == 0), stop=(g == ng - 1),
                        skip_group_check=True), reads=[wbuf, b_src], writes=[pb])
                P.op(ACT, lambda e, pt=pt, t0=t0, n=n: e.activation(out=dst_t[:ncols, t0:t0 + n], in_=pt[:ncols, :n], func=AF.Copy),
                     reads=[pb], writes=[dst_b])

        def norm192(nope_t, nope_b, rsq_t, rsq_b, scale):
            st_, sbf = sq.next()
            P.op(ACT, lambda e: e.activation(out=st_[:], in_=nope_t[:], func=AF.Square), reads=[nope_b], writes=[sbf])
            rt, rb = rs.next()
            for pi, (t0, n) in enumerate(pieces):
                pt, pb = pss.next()
                P.op(PE, lambda e, pt=pt, t0=t0, n=n: e.matmul(pt[:, :n], ones_f[:], st_[:, t0:t0 + n], start=True, stop=False,
                                                             skip_group_check=True), reads=[b_ones, sbf], writes=[pb])
                P.op(PE, lambda e, pt=pt, t0=t0, n=n: e.matmul(pt[:, :n], ones_f[0:64, :], rsq_t[:, t0:t0 + n], start=False, stop=True,
                                                             skip_group_check=True), reads=[b_ones, rsq_b], writes=[pb])
                P.op(ACT, lambda e, pt=pt, t0=t0, n=n: e.activation(
                    out=rt[:, t0:t0 + n], in_=pt[:, :n], func=AF.Sqrt, scale=1.0 / (192.0 * scale * scale),
                    bias=eps_t[:, (1 if scale != 1.0 else 0):(2 if scale != 1.0 else 1)]), reads=[pb, b_eps], writes=[rb])
            P.op(DVE, lambda e: e.reciprocal(out=rt[:], in_=rt[:]), reads=[rb], writes=[rb])
            return rt, rb

        def rope_store(src_t, src_b, gain, b_gain, rt, rb, dst):
            yt, yb = yr.next()
            P.op(DVE, lambda e: e.scalar_tensor_tensor(out=yt[:], in0=src_t[:], scalar=gain[:, 0:1], in1=rt[0:64, :],
                                                       op0=ALU.mult, op1=ALU.mult), reads=[src_b, b_gain, rb], writes=[yb])
            tt, tb = t1.next()
            for pi, (t0, n) in enumerate(pieces):
                pt, pb = ps.next()
                P.op(PE, lambda e, pt=pt, t0=t0, n=n: e.matmul(pt[0:64, :n], rotT[:], yt[:, t0:t0 + n], start=True, stop=True,
                                                             skip_group_check=True), reads=[b_rot, yb], writes=[pb])
                P.op(DVE, lambda e, pt=pt, t0=t0, n=n: e.tensor_tensor(out=tt[:, t0:t0 + n], in0=pt[0:64, :n], in1=sin_t[:, t0:t0 + n],
                                                                       op=ALU.mult), reads=[pb, b_sin], writes=[tb])
            P.op(DVE, lambda e: e.tensor_tensor(out=yt[:], in0=yt[:], in1=cos_t[:], op=ALU.mult), reads=[yb, b_cos], writes=[yb])
            ot, ob = obr.next()
            P.op(DVE, lambda e: e.tensor_tensor(out=ot[:], in0=yt[:], in1=tt[:], op=ALU.add), reads=[yb, tb], writes=[ob])
            P.op(SP, lambda e: e.dma_start(out=dst, in_=ot[:]), reads=[ob], writes=[Buf()], dma_key="st_obr%d" % obr.i)

        def nope_store(src_t, src_b, gain, b_gain, rt, rb, dst):
            ot, ob = obf.next()
            P.op(DVE, lambda e: e.scalar_tensor_tensor(out=ot[:], in0=src_t[:], scalar=gain[:, 0:1], in1=rt[:],
                                                       op0=ALU.mult, op1=ALU.mult), reads=[src_b, b_gain, rb], writes=[ob])
            P.op(SP, lambda e: e.dma_start(out=dst, in_=ot[:]), reads=[ob], writes=[Buf()], dma_key="st_obf%d" % obf.i)

        for h in range(NM):
            wt, wbuf = wq.next()
            P.op(POOL, lambda e, wt=wt, h=h: e.dma_start(
                out=wt[:], in_=d["w_uq"][:, h * 192:(h + 1) * 192].rearrange("(c p) n -> p c n", p=128)),
                writes=[wbuf], dma_key="wq%d" % wq.i)
            nt, nb = rawn.next()
            rt_, rb_ = rawr.next()
            up_fm(wt, wbuf, 0, 128, cqn, b_cqn, NQ, nt, nb)
            up_fm(wt, wbuf, 128, 64, cqn, b_cqn, NQ, rt_, rb_)
            st_, sbf = sq.next()
            P.op(ACT, lambda e, st_=st_, rt_=rt_: e.activation(out=st_[0:64, :], in_=rt_[:], func=AF.Square), reads=[rb_], writes=[sbf])
            class _V:
                def __init__(self, t):
                    self.t = t

                def __getitem__(self, k):
                    return self.t[0:64, k[1]]
            rst, rsb = norm192(nt, nb, _V(st_), sbf, 192.0 ** -0.5)
            nope_store(nt, nb, gqn, b_gqn, rst, rsb, d["qn"][NF + NS + h])
            rope_store(rt_, rb_, gqr, b_gqr, rst, rsb, d["qr"][h])
            kt, kb = rawn.next()
            up_fm(wkv, b_wkv, h * 256, 128, ckvn, b_ckvn, NK, kt, kb)
            rst, rsb = norm192(kt, kb, sqkr, b_sqkr, 1.0)
            nope_store(kt, kb, gkn, b_gkn, rst, rsb, d["kn"][NF + NS + h])
            rope_store(krr, b_krr, gkr, b_gkr, rst, rsb, d["kr"][h])
            for j in range(NJ):
                pt, pb = ps.next()
                for g in range(NK):
                    P.op(PE, lambda e, pt=pt, g=g, j=j, h=h: e.matmul(
                        pt[:, :128], ckvn[:, g, j * 128:(j + 1) * 128], wkv[:, g, h * 256 + 128:h * 256 + 256],
                        start=(g == 0), stop=(g == NK - 1), skip_group_check=True), reads=[b_ckvn, b_wkv], writes=[pb])
                vt, vb = vst.next()
                P.op(ACT, lambda e, vt=vt, pt=pt: e.activation(out=vt[:], in_=pt[:, :128], func=AF.Copy), reads=[pb], writes=[vb])
                P.op(SP, lambda e, vt=vt, j=j, h=h: e.dma_start(out=d["v"][NF + NS + h, j], in_=vt[:]), reads=[vb], writes=[Buf()],
                     dma_key="st_vm%d" % vst.i)
        P.barrier()
        P.flush()


def phase2(nc, P, cfg, d):
    T, NJ, NB, S, pieces = cfg.T, cfg.NJ, cfg.NB, cfg.S, cfg.pieces
    with contextlib.ExitStack() as st:
        def sb(name, shape, dt):
            return st.enter_context(nc.sbuf_tensor(uname(name), shape, dt))

        def const(name, shape, dt):
            t = sb("c_" + name, shape, dt)
            b = Buf(name)
            P.op(SP, lambda e: e.dma_start(out=t[:], in_=d[name]), writes=[b], dma_key="c_" + name)
            return t, b

        ones_f, b_onesf = const("ones_f", [128, 128], F32)
        ones_b, b_onesb = const("ones_b", [128, 128], BF16)
        ident_b, b_ident = const("ident_b", [128, 128], BF16)
        negT_b, b_negT = const("negT_b", [128, 128], BF16)
        tri_f, b_tri = const("tri_f", [128, 128], F32)
        bstrict, b_bstr = const("bstrict", [128, 128], F32)
        mask_le, b_mle = const("mask_le", [128, NCORES, 128], BF16)
        mask_lt, b_mlt = const("mask_lt", [128, NCORES, 128], BF16)
        sel, b_sel = const("sel", [128, NCORES], F32)
        gout, b_gout = const("gout", [128, NH], F32)
        lfg = sb("lfg", [128, NB, NF], F32)
        b_lfg = Buf()
        P.op(SP, lambda e: e.dma_start(out=lfg[:], in_=d["lf_g"].rearrange("g p h -> p g h")), writes=[b_lfg], dma_key="lfg")
        cst = sb("cst", [128, 4], F32)
        b_cst = Buf()
        P.op(DVE, lambda e: e.memset(cst[:, 0:1], 1.0), writes=[b_cst])
        P.op(DVE, lambda e: e.memset(cst[:, 1:2], 0.0), writes=[b_cst])

        ogT = sb("ogT", [128, NH, T], BF16)
        b_og = Buf("ogT")
        Qt = Ring(st, nc, "Qt", [128, T], BF16, 2)
        Qrt = Ring(st, nc, "Qrt", [64, T], BF16, 2)
        Gt = Ring(st, nc, "Gt", [128, T], BF16, 2)
        Pt = Ring(st, nc, "Pt", [128, 128], BF16, 4)
        Et = Ring(st, nc, "Et", [128, 128], F32, 3)
        Lt = Ring(st, nc, "Lt", [128, 128], BF16, 3)
        At = Ring(st, nc, "At", [128, 128], F32, 3)
        Cb = Ring(st, nc, "Cb", [128, 128], F32, 2)
        f1 = Ring(st, nc, "f1", [128, T], F32, 2)
        f2 = Ring(st, nc, "f2", [128, T], F32, 2)
        f3 = Ring(st, nc, "f3", [128, T], F32, 2)
        cum_sb = sb("cum_sb", [128, NB], F32)
        A_sb = sb("A_sb", [128, 128], F32)
        bown = sb("bown", [128, NJ], F32)
        btmp = sb("btmp", [128, NJ, NCORES], F32)
        bias_all = sb("bias_all", [128, NB, NJ], F32)
        b_cum, b_A, b_bown, b_btmp, b_bias = Buf(), Buf(), Buf(), Buf(), Buf()

        pS = Ring(st, nc, "pS", [128, 512], F32, 2, psum=True)
        pC = Ring(st, nc, "pC", [128, 512], F32, 1, psum=True)
        pO = [st.enter_context(nc.psum_tensor(uname("pO%d" % i), [128, 512], F32)) for i in range(len(pieces))]
        pL = [st.enter_context(nc.psum_tensor(uname("pL%d" % i), [128, 512], F32)) for i in range(len(pieces))]
        b_pO = [Buf() for _ in pieces]
        b_pL = [Buf() for _ in pieces]

        st2 = contextlib.ExitStack()
        Kt = Ring(st2, nc, "Kt", [128, S], BF16, 2)
        Krt = Ring(st2, nc, "Krt", [64, S], BF16, 1)
        Vt = Ring(st2, nc, "Vt", [128, NB, 128], BF16, 2)

        def ocol(j):
            return j // 4, (j % 4) * 128

        def load_head(h):
            kt, kb = Kt.next()
            P.op(SP, lambda e: e.dma_start(out=kt[:], in_=d["kn_g"][h]), writes=[kb], dma_key="K%d" % Kt.i)
            vt, vb = Vt.next()
            P.op(SP, lambda e: e.dma_start(out=vt[:], in_=d["v_g"][h].rearrange("g p v -> p g v")), writes=[vb], dma_key="V%d" % Vt.i)
            qt, qb = Qt.next()
            P.op(SP, lambda e: e.dma_start(out=qt[:], in_=d["qn"][h]), writes=[qb], dma_key="Q%d" % Qt.i)
            gt, gb = Gt.next()
            P.op(SP, lambda e: e.dma_start(out=gt[:], in_=d["gate"][h]), writes=[gb], dma_key="G%d" % Gt.i)
            res = dict(kt=kt, kb=kb, vt=vt, vb=vb, qt=qt, qb=qb, gt=gt, gb=gb)
            if h >= NF + NS:
                m = h - NF - NS
                krt, krb = Krt.next()
                P.op(SP, lambda e: e.dma_start(out=krt[:], in_=d["kr_g"][m]), writes=[krb], dma_key="Kr")
                qrt, qrb = Qrt.next()
                P.op(SP, lambda e: e.dma_start(out=qrt[:], in_=d["qr"][m]), writes=[qrb], dma_key="Qr%d" % Qrt.i)
                res.update(krt=krt, krb=krb, qrt=qrt, qrb=qrb)
            return res

        def fox_bias(h):
            lh = lfg[:, :, h]
            pa, pab = pS.next()
            P.op(PE, lambda e: e.matmul(pa[0:NB, 0:128], lh, ones_f[:], start=True, stop=True, skip_group_check=True),
                 reads=[b_lfg, b_onesf], writes=[pab])
            P.op(ACT, lambda e: e.activation(out=A_sb[0:NB, :], in_=pa[0:NB, 0:128], func=AF.Copy), reads=[pab], writes=[b_A])
            pc, pcb = pS.next()
            P.op(PE, lambda e: e.matmul(pc[:, 0:NB], tri_f[:], lh, start=True, stop=False, skip_group_check=True),
                 reads=[b_lfg, b_tri], writes=[pcb])
            P.op(PE, lambda e: e.matmul(pc[:, 0:NB], A_sb[0:NB, :], bstrict[0:NB, 0:NB], start=False, stop=True, skip_group_check=True),
                 reads=[b_A, b_bstr], writes=[pcb])
            P.op(ACT, lambda e: e.activation(out=cum_sb[:], in_=pc[:, 0:NB], func=AF.Copy), reads=[pcb], writes=[b_cum])
            po, pob = pC.next()
            P.op(PE, lambda e: e.matmul(po[:, 0:NB], A_sb[0:NB, :], bstrict[0:NB, 0:NB], start=True, stop=True, skip_group_check=True),
                 reads=[b_A, b_bstr], writes=[pob])
            P.op(DVE, lambda e: e.tensor_tensor(out=btmp[:], in0=po[:, 0:NB].rearrange("p (j r) -> p j r", r=NCORES),
                                                in1=sel[:].unsqueeze(1).to_broadcast([128, NJ, NCORES]), op=ALU.mult),
                 reads=[pob, b_sel], writes=[b_btmp])
            P.op(DVE, lambda e: e.tensor_reduce(out=bown[:], in_=btmp[:], axis=AX.X, op=ALU.add), reads=[b_btmp], writes=[b_bown])
            P.op(DVE, lambda e: e.tensor_tensor(out=bias_all[:], in0=bown[:].unsqueeze(1).to_broadcast([128, NB, NJ]),
                                                in1=cum_sb[:].unsqueeze(2).to_broadcast([128, NB, NJ]), op=ALU.subtract),
                 reads=[b_bown, b_cum], writes=[b_bias])

        def softmax_head(h, L):
            is_fox = h < NF
            is_mla = h >= NF + NS
            if is_fox:
                fox_bias(h)
            for j in range(NJ):
                bi, c0 = ocol(j)
                qs = slice(j * 128, (j + 1) * 128)
                ng = NCORES * j + NCORES
                for g in range(ng):
                    ks = slice(g * 128, (g + 1) * 128)
                    diag = g >= NCORES * j
                    pt, pb = pS.next()
                    last = not (is_mla or diag)
                    P.op(PE, lambda e, pt=pt, ks=ks, qs=qs, last=last: e.matmul(
                        pt[:, 0:128], L["kt"][:, ks], L["qt"][:, qs], start=True, stop=last, skip_group_check=True),
                        reads=[L["kb"], L["qb"]], writes=[pb])
                    if is_mla:
                        P.op(PE, lambda e, pt=pt, ks=ks, qs=qs, diag=diag: e.matmul(
                            pt[:, 0:128], L["krt"][:, ks], L["qrt"][:, qs], start=False, stop=not diag, skip_group_check=True),
                            reads=[L["krb"], L["qrb"]], writes=[pb])
                    if diag:
                        P.op(PE, lambda e, pt=pt, g=g, j=j: e.matmul(
                            pt[:, 0:128], ident_b[:], mask_le[:, g - NCORES * j, :], start=False, stop=True, skip_group_check=True),
                            reads=[b_ident, b_mle], writes=[pb])
                    p_t, p_b = Pt.next()
                    if is_fox:
                        P.op(ACT, lambda e, pt=pt, p_t=p_t, g=g, j=j: e.activation(
                            out=p_t[:], in_=pt[:, 0:128], func=AF.Exp, bias=bias_all[:, g, j:j + 1]),
                            reads=[pb, b_bias], writes=[p_b])
                    else:
                        P.op(ACT, lambda e, pt=pt, p_t=p_t: e.activation(out=p_t[:], in_=pt[:, 0:128], func=AF.Exp),
                             reads=[pb], writes=[p_b])
                    P.op(PE, lambda e, p_t=p_t, g=g, bi=bi, c0=c0, ng=ng: e.matmul(
                        pO[bi][:, c0:c0 + 128], L["vt"][:, g, :], p_t[:], start=(g == 0), stop=(g == ng - 1), skip_group_check=True),
                        reads=[L["vb"], p_b], writes=[b_pO[bi]])
                    P.op(PE, lambda e, p_t=p_t, g=g, bi=bi, c0=c0, ng=ng: e.matmul(
                        pL[bi][:, c0:c0 + 128], ones_b[:], p_t[:], start=(g == 0), stop=(g == ng - 1), skip_group_check=True),
                        reads=[b_onesb, p_b], writes=[b_pL[bi]])

        def sb_head(h, L):
            for j in range(NJ):
                bi, c0 = ocol(j)
                qs = slice(j * 128, (j + 1) * 128)
                ng = NCORES * j + NCORES
                cb_t, cb_b = Cb.next()
                P.op(DVE, lambda e, cb_t=cb_t: e.memset(cb_t[:], 0.0), writes=[cb_b])
                for g in range(ng - 1, -1, -1):
                    ks = slice(g * 128, (g + 1) * 128)
                    diag = g >= NCORES * j
                    pt, pb = pS.next()
                    P.op(PE, lambda e, pt=pt, ks=ks, qs=qs: e.matmul(
                        pt[:, 0:128], L["kt"][:, ks], L["qt"][:, qs], start=True, stop=False, skip_group_check=True),
                        reads=[L["kb"], L["qb"]], writes=[pb])
                    if diag:
                        P.op(PE, lambda e, pt=pt, g=g, j=j: e.matmul(
                            pt[:, 0:128], ident_b[:], mask_lt[:, g - NCORES * j, :], start=False, stop=False, skip_group_check=True),
                            reads=[b_ident, b_mlt], writes=[pb])
                    e_t, e_b = Et.next()
                    P.op(ACT, lambda e, pt=pt, e_t=e_t: e.activation(out=e_t[:], in_=pt[:, 0:128], func=AF.Exp), reads=[pb], writes=[e_b])
                    l_t, l_b = Lt.next()
                    P.op(ACT, lambda e, e_t=e_t, l_t=l_t: e.activation(out=l_t[:], in_=e_t[:], func=AF.Ln, bias=cst[:, 0:1]),
                         reads=[e_b, b_cst], writes=[l_b])
                    P.op(PE, lambda e, pt=pt, l_t=l_t: e.matmul(pt[:, 0:128], negT_b[:], l_t[:], start=False, stop=True, skip_group_check=True),
                         reads=[b_negT, l_b], writes=[pb])
                    pc, pcb = pC.next()
                    P.op(PE, lambda e, pc=pc, l_t=l_t: e.matmul(pc[:, 0:128], ones_b[:], l_t[:], start=True, stop=True, skip_group_check=True),
                         reads=[b_onesb, l_b], writes=[pcb])
                    a_t, a_b = At.next()
                    P.op(DVE, lambda e, a_t=a_t, pt=pt, cb_t=cb_t: e.tensor_tensor(out=a_t[:], in0=pt[:, 0:128], in1=cb_t[:], op=ALU.subtract),
                         reads=[pb, cb_b], writes=[a_b])
                    p_t, p_b = Pt.next()
                    P.op(ACT, lambda e, a_t=a_t, p_t=p_t: e.activation(out=p_t[:], in_=a_t[:], func=AF.Exp), reads=[a_b], writes=[p_b])
                    P.op(PE, lambda e, p_t=p_t, g=g, bi=bi, c0=c0, ng=ng: e.matmul(
                        pO[bi][:, c0:c0 + 128], L["vt"][:, g, :], p_t[:], start=(g == ng - 1), stop=(g == 0), skip_group_check=True),
                        reads=[L["vb"], p_b], writes=[b_pO[bi]])
                    P.op(DVE, lambda e, cb_t=cb_t, pc=pc: e.tensor_tensor(out=cb_t[:], in0=cb_t[:], in1=pc[:, 0:128], op=ALU.add),
                         reads=[cb_b, pcb], writes=[cb_b])

        def epilogue(h, L, softmax):
            osq, osq_b = f1.next()
            tl, tl_b = f2.next()
            yv, yv_b = f3.next()
            for pi, (t0, n) in enumerate(pieces):
                P.op(ACT, lambda e, pi=pi, t0=t0, n=n: e.activation(out=osq[:, t0:t0 + n], in_=pO[pi][:, :n], func=AF.Square),
                     reads=[b_pO[pi]], writes=[osq_b])
                if softmax:
                    P.op(ACT, lambda e, pi=pi, t0=t0, n=n: e.activation(out=tl[:, t0:t0 + n], in_=pL[pi][:, :n], func=AF.Square,
                                                                         scale=math.sqrt(EPS)), reads=[b_pL[pi]], writes=[tl_b])
                pt, pb = pS.next()
                P.op(PE, lambda e, pt=pt, t0=t0, n=n: e.matmul(pt[:, :n], ones_f[:], osq[:, t0:t0 + n], start=True, stop=True,
                                                             skip_group_check=True), reads=[b_onesf, osq_b], writes=[pb])
                if softmax:
                    P.op(DVE, lambda e, pt=pt, t0=t0, n=n: e.scalar_tensor_tensor(
                        out=tl[:, t0:t0 + n], in0=pt[:, :n], scalar=1.0 / HD, in1=tl[:, t0:t0 + n], op0=ALU.mult, op1=ALU.add),
                        reads=[pb, tl_b], writes=[tl_b])
                else:
                    P.op(DVE, lambda e, pt=pt, t0=t0, n=n: e.tensor_scalar(
                        out=tl[:, t0:t0 + n], in0=pt[:, :n], scalar1=1.0 / HD, scalar2=EPS, op0=ALU.mult, op1=ALU.add),
                        reads=[pb], writes=[tl_b])
            P.op(ACT, lambda e: e.activation(out=tl[:], in_=tl[:], func=AF.Sqrt), reads=[tl_b], writes=[tl_b])
            P.op(DVE, lambda e: e.reciprocal(out=tl[:], in_=tl[:]), reads=[tl_b], writes=[tl_b])
            for pi, (t0, n) in enumerate(pieces):
                P.op(DVE, lambda e, pi=pi, t0=t0, n=n: e.scalar_tensor_tensor(
                    out=yv[:, t0:t0 + n], in0=pO[pi][:, :n], scalar=gout[:, h:h + 1], in1=tl[:, t0:t0 + n], op0=ALU.mult, op1=ALU.mult),
                    reads=[b_pO[pi], b_gout, tl_b], writes=[yv_b])
            P.op(DVE, lambda e: e.tensor_tensor(out=ogT[:, h, :], in0=yv[:], in1=L["gt"][:], op=ALU.mult),
                 reads=[yv_b, L["gb"]], writes=[b_og])

        for h in range(NH):
            L = load_head(h)
            if NF <= h < NF + NS:
                sb_head(h, L)
                epilogue(h, L, False)
            else:
                softmax_head(h, L)
                epilogue(h, L, True)

        P.barrier()
        P.flush()
        st2.close()

        wo = Ring(st, nc, "wo", [128, NH, 128], BF16, 2)
        xr = Ring(st, nc, "xr", [128, T], F32, 2)
        xo = Ring(st, nc, "xo", [128, T], F32, 2)
        for oc in range(NCH):
            wt, wbuf = wo.next()
            P.op(POOL, lambda e, wt=wt, oc=oc: e.dma_start(
                out=wt[:], in_=d["w_out"][:, oc * 128:(oc + 1) * 128].rearrange("(h p) n -> p h n", p=128)),
                writes=[wbuf], dma_key="wo%d" % wo.i)
            xt, xb = xr.next()
            P.op(SP, lambda e, xt=xt, oc=oc: e.dma_start(out=xt[:], in_=d["xT"][:, oc, :]), writes=[xb], dma_key="xr%d" % xr.i)
            ot, ob = xo.next()
            for pi, (t0, n) in enumerate(pieces):
                pt, pb = pS.next()
                for h in range(NH):
                    P.op(PE, lambda e, pt=pt, wt=wt, h=h, t0=t0, n=n: e.matmul(
                        pt[:, :n], wt[:, h, :], ogT[:, h, t0:t0 + n], start=(h == 0), stop=(h == NH - 1), skip_group_check=True),
                        reads=[wbuf, b_og], writes=[pb])
                P.op(DVE, lambda e, pt=pt, xt=xt, ot=ot, t0=t0, n=n: e.tensor_tensor(
                    out=ot[:, t0:t0 + n], in0=pt[:, :n], in1=xt[:, t0:t0 + n], op=ALU.add), reads=[pb, xb], writes=[ob])
            P.op(SP, lambda e, ot=ot, oc=oc: e.dma_start(out=d["xT_out"][:, oc, :], in_=ot[:]), reads=[ob], writes=[Buf()],
                 dma_key="xo%d" % xo.i)
        P.barrier()
        P.flush()


A_IN = ["xT", "pos", "w_in", "w_uq", "w_ukv", "gin", "gq_f", "gk_f", "bf_b", "gcq", "gckv", "gqm_n", "gqm_r", "gkm_n", "gkm_r",
        "ones_f", "rotT", "invf"]
A_OUT = ["qn", "qr", "gate", "kn", "kr", "v", "lf"]
A_INT = ["cq_raw", "ckv_raw", "kr_raw"]
B_IN = ["xT", "qn", "qr", "gate", "kn_g", "kr_g", "v_g", "lf_g", "w_out", "gout", "ones_f", "ones_b", "ident_b", "negT_b",
        "tri_f", "bstrict", "mask_le", "mask_lt", "sel"]
B_OUT = ["xT_out"]


def build_A(cfg):
    nc = bass.Bass("TRN2", target_bir_lowering=False)
    d = declare(nc, cfg, A_IN, A_OUT, A_INT)
    P = Prog(nc)
    phase1(nc, P, cfg, d)
    phase1b(nc, P, cfg, d)
    return nc


def build_B(cfg):
    nc = bass.Bass("TRN2", target_bir_lowering=False)
    d = declare(nc, cfg, B_IN, B_OUT)
    P = Prog(nc)
    phase2(nc, P, cfg, d)
    return nc


def kernel(x, positions, norm_in, w_in, b_f, q_norm_fox, k_norm_fox, cq_norm, w_uq, ckv_norm, w_ukv,
           q_norm_mla, k_norm_mla, out_norm, w_out):
    inp = dict(x=np.asarray(x), positions=np.asarray(positions), norm_in=np.asarray(norm_in), w_in=np.asarray(w_in),
               b_f=np.asarray(b_f), q_norm_fox=np.asarray(q_norm_fox), k_norm_fox=np.asarray(k_norm_fox),
               cq_norm=np.asarray(cq_norm), w_uq=np.asarray(w_uq), ckv_norm=np.asarray(ckv_norm), w_ukv=np.asarray(w_ukv),
               q_norm_mla=np.asarray(q_norm_mla), k_norm_mla=np.asarray(k_norm_mla), out_norm=np.asarray(out_norm),
               w_out=np.asarray(w_out))
    seq = inp["x"].shape[1]
    cfg = Cfg(seq // (NCORES * 128))
    T, NJ, NB, S = cfg.T, cfg.NJ, cfg.NB, cfg.S
    cst = host_consts(cfg)
    ccs = [core_consts(cfg, c) for c in range(NCORES)]
    toks = [own_tokens(cfg, c) for c in range(NCORES)]
    x0 = inp["x"][0]
    xT = [np.ascontiguousarray(x0[toks[c]].T.reshape(NCH, 128, T).transpose(1, 0, 2)) for c in range(NCORES)]
    pos = [np.ascontiguousarray(inp["positions"][0][toks[c]].reshape(1, T).astype(np.int32)) for c in range(NCORES)]
    ncA = build_A(cfg)
    ncB = build_B(cfg)
    cores = list(range(NCORES))
    for l in range(inp["w_in"].shape[0]):
        lw = layer_weights(inp, l)
        mapsA = []
        for c in cores:
            m = {"xT": xT[c], "pos": pos[c]}
            for n in A_IN:
                if n not in m:
                    m[n] = lw[n] if n in lw else cst[n]
            mapsA.append(m)
        rA = run_bass_kernel_spmd(ncA, mapsA, core_ids=cores).results
        del mapsA
        kn_g = np.empty((NH, 128, S), ml_dtypes.bfloat16)
        kr_g = np.empty((NM, 64, S), ml_dtypes.bfloat16)
        v_g = np.empty((NH, NB, 128, 128), ml_dtypes.bfloat16)
        lf_g = np.empty((NB, 128, NF), np.float32)
        for c in cores:
            kn_g[:, :, toks[c]] = rA[c]["kn"]
            kr_g[:, :, toks[c]] = rA[c]["kr"]
            for j in range(NJ):
                v_g[:, NCORES * j + c] = rA[c]["v"][:, j]
                lf_g[NCORES * j + c] = rA[c]["lf"][j]
        mapsB = []
        for c in cores:
            m = {"xT": xT[c], "qn": rA[c]["qn"], "qr": rA[c]["qr"], "gate": rA[c]["gate"],
                 "kn_g": kn_g, "kr_g": kr_g, "v_g": v_g, "lf_g": lf_g}
            m.update(ccs[c])
            for n in B_IN:
                if n not in m:
                    m[n] = lw[n] if n in lw else cst[n]
            mapsB.append(m)
        del rA
        rB = run_bass_kernel_spmd(ncB, mapsB, core_ids=cores).results
        del mapsB
        xT = [np.ascontiguousarray(rB[c]["xT_out"]) for c in cores]
    out = np.empty((1, seq, D), np.float32)
    for c in cores:
        out[0, toks[c]] = xT[c].transpose(1, 0, 2).reshape(D, T).T
    return out
```
